# Optimizing a Trainium2 kernel written in Bass

```python
import jax, jax.numpy as jnp
from jax import lax
import numpy as np

D_MODEL = 1024
BATCH = 1
SEQ = 16384
DEPTH = 4

CTX_LEN = 256
GRID_W = 64
N_MIXERS = 2
CONV_KERNEL = 31
CONV_PAD = CONV_KERNEL // 2
RET_HEADS = 4
RET_DK = D_MODEL // RET_HEADS
RET_DV = 2 * RET_DK
RET_QK_W = RET_HEADS * RET_DK
RET_V_W = RET_HEADS * RET_DV
RET_IN_W = 2 * RET_QK_W + 3 * RET_V_W
RET_CHUNK = 128
ROPE_BASE = 10000.0
D_FF = -(-8 * D_MODEL // (3 * 256)) * 256
NORM_EPS = 1e-6
LN_EPS = 1e-5

kernel_name = "hybrid_conformer_retention_dit_backbone"


def rms_norm(x, g):
    xf = x.astype(jnp.float32)
    y = xf * lax.rsqrt(jnp.mean(jnp.square(xf), axis=-1, keepdims=True) + NORM_EPS)
    return y.astype(x.dtype) * g


def layer_norm(x, g, b):
    xf = x.astype(jnp.float32)
    mu = jnp.mean(xf, axis=-1, keepdims=True)
    var = jnp.mean(jnp.square(xf - mu), axis=-1, keepdims=True)
    return ((xf - mu) * lax.rsqrt(var + LN_EPS)).astype(x.dtype) * g + b


def swiglu(h, w_in, w_out):
    gt, up = jnp.split(h @ w_in, 2, axis=-1)
    return (jax.nn.silu(gt) * up) @ w_out


def conv_module(h, pw1_w, pw1_b, dw_w, dw_b, ln_g, ln_b, pw2_w, pw2_b):
    a, gt = jnp.split(h @ pw1_w + pw1_b, 2, axis=-1)
    u = a * jax.nn.sigmoid(gt)
    u = lax.conv_general_dilated(
        u, dw_w[:, None, :].astype(u.dtype), window_strides=(1,),
        padding=[(CONV_PAD, CONV_PAD)], dimension_numbers=("NWC", "WIO", "NWC"),
        feature_group_count=D_MODEL) + dw_b
    u = jax.nn.silu(layer_norm(u, ln_g, ln_b))
    return u @ pw2_w + pw2_b


def axial_rope(n_rows):
    row = jnp.repeat(jnp.arange(n_rows, dtype=jnp.float32), GRID_W)
    col = jnp.tile(jnp.arange(GRID_W, dtype=jnp.float32), n_rows)
    quarter = RET_DK // 4
    inv = ROPE_BASE ** (-jnp.arange(quarter, dtype=jnp.float32) / quarter)
    ar = row[:, None] * inv
    ac = col[:, None] * inv
    ang = jnp.concatenate([ar, ar, ac, ac], axis=-1)
    return jnp.cos(ang), jnp.sin(ang)


def apply_rope(t, cos, sin):
    r1, r2, c1, c2 = jnp.split(t, 4, axis=-1)
    rot = jnp.concatenate([-r2, r1, -c2, c1], axis=-1)
    return t * cos[None, :, None, :].astype(t.dtype) + rot * sin[None, :, None, :].astype(t.dtype)


def retention_scan(q, k, v, gamma, state0):
    b, h, t, _ = q.shape
    n = t // RET_CHUNK

    def chunks(a):
        return jnp.moveaxis(a.astype(jnp.float32).reshape(b, h, n, RET_CHUNK, a.shape[-1]), 2, 0)

    log_g = jnp.log(gamma.astype(jnp.float32))
    idx = jnp.arange(RET_CHUNK, dtype=jnp.float32)
    rel = idx[:, None] - idx[None, :]
    decay_intra = jnp.where(rel >= 0, jnp.exp(jnp.maximum(rel, 0.0)[None] * log_g[:, None, None]), 0.0)
    q_decay = jnp.exp((idx + 1.0)[None, :] * log_g[:, None])
    k_decay = jnp.exp((RET_CHUNK - 1.0 - idx)[None, :] * log_g[:, None])
    chunk_decay = jnp.exp(RET_CHUNK * log_g)[:, None, None]

    def step(state, qkv):
        qc, kc, vc = qkv
        scores = jnp.einsum('bhid,bhjd->bhij', qc, kc) * decay_intra
        out = (jnp.einsum('bhij,bhjv->bhiv', scores, vc)
               + jnp.einsum('bhid,bhdv->bhiv', qc, state) * q_decay[None, :, :, None])
        state = state * chunk_decay + jnp.einsum('bhjd,bhjv->bhdv', kc * k_decay[None, :, :, None], vc)
        return state, out

    state, out = lax.scan(step, state0, (chunks(q), chunks(k), chunks(v)))
    out = jnp.moveaxis(out, 0, 2).reshape(b, h, t, -1)
    return out, state


def head_norm(y):
    y = y * lax.rsqrt(jnp.mean(jnp.square(y), axis=-1, keepdims=True) + NORM_EPS)
    b, h, t, dv = y.shape
    return y.transpose(0, 2, 1, 3).reshape(b, t, h * dv)


def retention_mixer(h_lat, h_ctx, w_in, log2_eps, w_out, cos, sin, need_ctx_out):
    gamma = 1.0 - jnp.exp2(log2_eps.astype(jnp.float32))

    def project(h, rope):
        b, t, _ = h.shape
        q, k, v, gf, gb = jnp.split(
            h @ w_in, [RET_QK_W, 2 * RET_QK_W, 2 * RET_QK_W + RET_V_W, 2 * RET_QK_W + 2 * RET_V_W], axis=-1)
        q = q.reshape(b, t, RET_HEADS, RET_DK)
        k = k.reshape(b, t, RET_HEADS, RET_DK)
        if rope:
            q = apply_rope(q, cos, sin)
            k = apply_rope(k, cos, sin)
        k = k * (RET_DK ** -0.5)
        v = v.reshape(b, t, RET_HEADS, RET_DV)
        heads = lambda a: a.transpose(0, 2, 1, 3)
        return heads(q), heads(k), heads(v), gf, gb

    flip = lambda a: jnp.flip(a, axis=2)
    q_c, k_c, v_c, gf_c, gb_c = project(h_ctx, False)
    q_l, k_l, v_l, gf_l, gb_l = project(h_lat, True)
    zero = jnp.zeros((h_lat.shape[0], RET_HEADS, RET_DK, RET_DV), jnp.float32)

    yf_c, s_f = retention_scan(q_c, k_c, v_c, gamma[0], zero)
    yb_c, s_b = retention_scan(flip(q_c), flip(k_c), flip(v_c), gamma[1], zero)
    yf_l, _ = retention_scan(q_l, k_l, v_l, gamma[0], s_f)
    yb_l, _ = retention_scan(flip(q_l), flip(k_l), flip(v_l), gamma[1], s_b)

    def merge(yf, yb, gf, gb):
        y = jax.nn.silu(gf) * head_norm(yf).astype(gf.dtype) + jax.nn.silu(gb) * head_norm(yb).astype(gb.dtype)
        return y @ w_out

    out_lat = merge(yf_l, flip(yb_l), gf_l, gb_l)
    out_ctx = merge(yf_c, flip(yb_c), gf_c, gb_c) if need_ctx_out else None
    return out_lat, out_ctx


def setup_inputs(seed: int = 0) -> dict:
    key = jax.random.key(seed)
    ks = jax.random.split(key, 24)
    n_conv = (DEPTH + N_MIXERS - 1) // N_MIXERS
    n_ret = DEPTH // N_MIXERS
    nrm = lambda k, shape, fan_in, s=1.0: jax.random.normal(k, shape, jnp.float32) * (s * fan_in ** -0.5)
    gain = lambda k, shape: 1.0 + 0.02 * jax.random.normal(k, shape, jnp.float32)
    bias = lambda k, shape: 0.02 * jax.random.normal(k, shape, jnp.float32)
    log2_eps = (-5.0 - jnp.arange(RET_HEADS, dtype=jnp.float32))[None, None, :] \
        + 0.1 * jax.random.normal(ks[17], (n_ret, 2, RET_HEADS), jnp.float32)
    return {
        "x": jax.random.normal(ks[0], (BATCH, SEQ, D_MODEL), jnp.float32),
        "c": jax.random.normal(ks[1], (BATCH, D_MODEL), jnp.float32),
        "ctx": jax.random.normal(ks[2], (BATCH, CTX_LEN, D_MODEL), jnp.float32),
        "c_ctx": jax.random.normal(ks[3], (D_MODEL,), jnp.float32),
        "mod_w": nrm(ks[4], (DEPTH, D_MODEL, 6 * D_MODEL), D_MODEL, 0.5),
        "mod_b": bias(ks[5], (DEPTH, 6 * D_MODEL)),
        "norm1_g": gain(ks[6], (DEPTH, D_MODEL)),
        "norm2_g": gain(ks[7], (DEPTH, D_MODEL)),
        "conv_pw1_w": nrm(ks[8], (n_conv, D_MODEL, 2 * D_MODEL), D_MODEL),
        "conv_pw1_b": bias(ks[9], (n_conv, 2 * D_MODEL)),
        "conv_dw_w": nrm(ks[10], (n_conv, CONV_KERNEL, D_MODEL), CONV_KERNEL),
        "conv_dw_b": bias(ks[11], (n_conv, D_MODEL)),
        "conv_ln_g": gain(ks[12], (n_conv, D_MODEL)),
        "conv_ln_b": bias(ks[13], (n_conv, D_MODEL)),
        "conv_pw2_w": nrm(ks[14], (n_conv, D_MODEL, D_MODEL), D_MODEL),
        "conv_pw2_b": bias(ks[15], (n_conv, D_MODEL)),
        "ret_w_in": nrm(ks[16], (n_ret, D_MODEL, RET_IN_W), D_MODEL),
        "ret_log2_eps": log2_eps,
        "ret_w_out": nrm(ks[18], (n_ret, RET_V_W, D_MODEL), RET_V_W),
        "ffn_w_in": nrm(ks[19], (DEPTH, D_MODEL, 2 * D_FF), D_MODEL),
        "ffn_w_out": nrm(ks[20], (DEPTH, D_FF, D_MODEL), D_FF),
        "final_norm_g": gain(ks[21], (D_MODEL,)),
    }


def reference(x, c, ctx, c_ctx, mod_w, mod_b, norm1_g, norm2_g,
              conv_pw1_w, conv_pw1_b, conv_dw_w, conv_dw_b, conv_ln_g, conv_ln_b, conv_pw2_w, conv_pw2_b,
              ret_w_in, ret_log2_eps, ret_w_out, ffn_w_in, ffn_w_out, final_norm_g):
    n_tok = x.shape[1]
    ROWS = n_tok // GRID_W
    cos, sin = axial_rope(ROWS)
    silu_c = jax.nn.silu(c)[:, None, :]
    silu_cc = jax.nn.silu(c_ctx)
    ctx_s = ctx

    for i in range(DEPTH):
        last = i == DEPTH - 1
        j = i // N_MIXERS
        m_lat = jnp.split(silu_c @ mod_w[i] + mod_b[i], 6, axis=-1)
        m_ctx = jnp.split(silu_cc @ mod_w[i] + mod_b[i], 6, axis=-1)
        h_lat = rms_norm(x, norm1_g[i]) * (1 + m_lat[1]) + m_lat[0]

        if i % N_MIXERS == 0:
            conv_args = (conv_pw1_w[j], conv_pw1_b[j], conv_dw_w[j], conv_dw_b[j],
                         conv_ln_g[j], conv_ln_b[j], conv_pw2_w[j], conv_pw2_b[j])
            y_lat = conv_module(h_lat, *conv_args)
            if not last:
                h_ctx = rms_norm(ctx_s, norm1_g[i]) * (1 + m_ctx[1]) + m_ctx[0]
                y_ctx = conv_module(h_ctx, *conv_args)
        else:
            h_ctx = rms_norm(ctx_s, norm1_g[i]) * (1 + m_ctx[1]) + m_ctx[0]
            y_lat, y_ctx = retention_mixer(h_lat, h_ctx, ret_w_in[j], ret_log2_eps[j], ret_w_out[j],
                                           cos, sin, not last)

        x = x + m_lat[2] * y_lat
        x = x + m_lat[5] * swiglu(rms_norm(x, norm2_g[i]) * (1 + m_lat[4]) + m_lat[3], ffn_w_in[i], ffn_w_out[i])
        if not last:
            ctx_s = ctx_s + m_ctx[2] * y_ctx
            ctx_s = ctx_s + m_ctx[5] * swiglu(rms_norm(ctx_s, norm2_g[i]) * (1 + m_ctx[4]) + m_ctx[3],
                                              ffn_w_in[i], ffn_w_out[i])

    return rms_norm(x, final_norm_g)
```

```python
import numpy as np
from contextlib import ExitStack
import concourse.bass as bass
import concourse.mybir as mybir
from concourse.bass_utils import run_bass_kernel_spmd

F32 = mybir.dt.float32
BF16 = mybir.dt.bfloat16
AF = mybir.ActivationFunctionType
ALU = mybir.AluOpType
AX = mybir.AxisListType

NCORES = 8
D = 1024
SEQ = 16384
TPC = SEQ // NCORES
CTX = 256
DEPTH = 4
DFF = 2816
NF = DFF // 128
HALO = 15
XW = TPC + 2 * HALO
NP = 41
ENGS = ('pe', 'act', 'dve', 'pool', 'sp')


class Prog:
    def __init__(self, nc):
        self.nc = nc
        self.ins = {e: [] for e in ENGS}
        self.lastw = {}
        self.readers = {}
        self.dcount = {}
        self.persist = set()

    def _deps(self, eng, reads, writes, is_dma, semkey=None):
        deps = {}

        def add(tok, typ):
            k, v = tok
            if k[0] == 'c' and k[1] == eng and not is_dma:
                if eng == 'pe' or typ == 'war':
                    return
            if is_dma and typ == 'waw' and k == ('d', semkey):
                return
            if deps.get(k, -1) < v:
                deps[k] = v
        for r in reads:
            t = self.lastw.get(r)
            if t is not None:
                add(t, 'raw')
        for w in writes:
            t = self.lastw.get(w)
            if t is not None:
                add(t, 'waw')
            for k, v in self.readers.get(w, {}).items():
                add((k, v), 'war')
        return deps

    def _commit(self, tok, reads, writes):
        k, v = tok
        for r in reads:
            d = self.readers.setdefault(r, {})
            if d.get(k, -1) < v:
                d[k] = v
        for w in writes:
            self.lastw[w] = tok
            self.readers[w] = {}

    def op(self, eng, fn, reads=(), writes=()):
        deps = self._deps(eng, reads, writes, False)
        idx = len(self.ins[eng])
        self.ins[eng].append(dict(fn=fn, deps=deps, dsem=None, signal=False))
        self._commit((('c', eng), idx), reads, writes)

    def dma(self, q, fn, reads=(), writes=(), sem=None):
        semkey = sem if sem is not None else (writes[0] if writes else reads[0])
        deps = self._deps(q, reads, writes, True, semkey)
        cnt = self.dcount.get(semkey, 0) + 16
        self.dcount[semkey] = cnt
        self.ins[q].append(dict(fn=fn, deps=deps, dsem=semkey, signal=False))
        self._commit((('d', semkey), cnt), reads, writes)

    def barrier(self):
        snap = {}
        for e in ENGS:
            if e == 'pool':
                continue
            for i in range(len(self.ins[e]) - 1, -1, -1):
                if self.ins[e][i]['dsem'] is None and self.ins[e][i]['fn'] is not None:
                    snap[('c', e)] = i
                    break
        for k, c in self.dcount.items():
            if k not in self.persist:
                snap[('d', k)] = c
        for e in ENGS:
            deps = {k: v for k, v in snap.items() if not (k[0] == 'c' and k[1] == e)}
            self.ins[e].append(dict(fn=None, deps=deps, dsem=None, signal=False))

    def emit(self, es):
        nc = self.nc
        for e in ENGS:
            for ins in self.ins[e]:
                for k, v in ins['deps'].items():
                    if k[0] == 'c':
                        self.ins[k[1]][v]['signal'] = True
        esem = {e: es.enter_context(nc.semaphore(f"s_{e}")) for e in ENGS}
        dsem = {}
        for i, k in enumerate(self.dcount):
            dsem[k] = es.enter_context(nc.semaphore(f"d_{i}"))
        sigval = {}
        for e in ENGS:
            c = 0
            for i, ins in enumerate(self.ins[e]):
                if ins['signal']:
                    c += 1
                    sigval[(e, i)] = c
        block = es.enter_context(nc.Block())

        def run(e, eng):
            waited = {}
            for ins in self.ins[e]:
                for k, v in ins['deps'].items():
                    if k[0] == 'c':
                        sem, val = esem[k[1]], sigval[(k[1], v)]
                    else:
                        sem, val = dsem[k[1]], v
                    if waited.get(k, -1) >= val:
                        continue
                    waited[k] = val
                    eng.wait_ge(sem, val)
                if ins['fn'] is None:
                    continue
                r = ins['fn'](eng)
                if ins['dsem'] is not None:
                    r.then_inc(dsem[ins['dsem']], 16)
                elif ins['signal']:
                    r.then_inc(esem[e], 1)

        @block.tensor
        def _(eng):
            run('pe', eng)

        @block.scalar
        def _(eng):
            run('act', eng)

        @block.vector
        def _(eng):
            run('dve', eng)

        @block.gpsimd
        def _(eng):
            run('pool', eng)

        @block.sync
        def _(eng):
            run('sp', eng)


class Rot:
    def __init__(self, items):
        self.items = items
        self.i = 0

    def next(self):
        r = self.items[self.i % len(self.items)]
        self.i += 1
        return r


FFN_GROUPS = [(0, 5), (5, 10), (10, 14), (14, 18), (18, 22)]


def piece_plan():
    pcs = []

    def conv(j):
        pcs.append([('k', 'pw1w', j, 0, 2048)])
        pcs.append([('k', 'pw2w', j, 0, 1024)])

    def ffn(i):
        for (f0, f1) in FFN_GROUPS:
            nf = f1 - f0
            pcs.append([('k', 'ffnwin', i, f0 * 128, nf * 128), ('k', 'ffnwin', i, DFF + f0 * 128, nf * 128),
                        ('n', 'ffnwout', i, f0 * 128, nf * 128)])

    def ret(j):
        for h in range(4):
            pcs.append([('k', 'retwin', j, h * 2048, 2048)])
            if h % 2 == 0:
                pcs.append([('n', 'retwout', j, h * 512, 1024)])

    for i in range(DEPTH):
        (conv if i % 2 == 0 else ret)(i // 2)
        ffn(i)
    return pcs


def block_words(b):
    return 128 * b[4]


def pack_pieces(ws, c):
    out = []
    for pc in piece_plan():
        for (kind, nm, l, a0, n) in pc:
            w = ws[nm][l]
            if kind == 'k':
                out.append(w[c * 128:(c + 1) * 128, a0:a0 + n].ravel())
            else:
                out.append(w[a0:a0 + n, c * 128:(c + 1) * 128].ravel())
    return np.ascontiguousarray(np.concatenate(out))


def build_program(stage=2 * DEPTH, final_norm=True, dbg=False):
    nc = bass.Bass("TRN2", target_bir_lowering=False)

    def din(n, s):
        return nc.dram_tensor(n, list(s), F32, kind="ExternalInput").ap()

    x_in = din("x", [TPC, D])
    xh_in = din("xh", [2 * HALO, D])
    ctx_in = din("ctx", [CTX, D])
    cc_in = din("cc", [128, 8, 2])
    modw_in = din("modw", [24, D, 128])
    modb_in = din("modb", [128, 192])
    g1_in = din("g1", [128, DEPTH, 8])
    g2_in = din("g2", [128, DEPTH, 8])
    gfin_in = din("gfin", [128, 8])
    pw1b_in = din("pw1b", [128, 2, 16])
    dww_in = din("dww", [128, 2, 8, 31])
    cvec_in = din("cvec", [128, 4, 2, 8])
    lg_in = din("lg", [128, 16])
    PLAN = piece_plan()
    pc_words = [sum(block_words(b) for b in pc) for pc in PLAN]
    pc_off = [0]
    for w_ in pc_words:
        pc_off.append(pc_off[-1] + w_)
    wpk_in = din("wpk", [pc_off[-1]])
    ident_in = din("ident", [128, 128])
    ptab_in = din("ptab", [128, 2, NP])
    m01_in = din("m01", [128, 2, 128])
    rope_in = din("rope", [128, 2, TPC])
    xtab_in = din("xtab", [128, 2, 2, 9])
    ohot_in = din("ohot", [128, 2, 8])
    maskh_in = din("maskh", [128, 2 * HALO])
    out = nc.dram_tensor("out", [TPC, D], F32, kind="ExternalOutput").ap()
    cdbg = nc.dram_tensor("cdbg", [CTX, D], F32, kind="ExternalOutput").ap() if dbg else None

    def dint(n, s, dt=F32):
        return nc.dram_tensor(n, list(s), dt, kind="Internal").ap()

    mod_bounce = dint("mod_bounce", [128, 48])
    mod_gath = dint("mod_gath", [NCORES * 128, 48])
    GB = dint("gbounce", [262144])
    GG = dint("ggather", [NCORES * 262144])
    GG2 = dint("ggather2", [NCORES * 131072])
    st_bounce = GB[0:262144].rearrange("(r c) -> r c", c=512)
    st_gath = GG[0:NCORES * 262144].rearrange("(r c) -> r c", c=512)
    halo_bounce = dint("halo_bounce", [128, 8 * 2 * HALO])
    halo_gath = dint("halo_gath", [NCORES * 128, 8 * 2 * HALO])
    u_dram = dint("u_dram", [2, 18, 128, 512], BF16)
    RG = [list(range(NCORES))]

    with ExitStack() as es:
        P = Prog(nc)
        P.persist.add('GB')

        uniq = [0]

        def sb(n, s, d, st=es):
            uniq[0] += 1
            return st.enter_context(nc.sbuf_tensor(f"s{uniq[0]}_{n}", list(s), d))

        def MM(o, lhsT, rhs, start, stop, reads, writes):
            P.op('pe', lambda e: e.matmul(o, lhsT, rhs, start=start, stop=stop), reads, writes)

        def TR(o, in_, idt, reads, writes):
            P.op('pe', lambda e: e.transpose(o, in_, idt), reads, writes)

        def ACT(o, in_, func, reads, writes, bias=None, scale=None, accum=None):
            kw = {}
            if bias is not None:
                kw['bias'] = bias
            if scale is not None:
                kw['scale'] = scale
            if accum is not None:
                kw['accum_out'] = accum
            P.op('act', lambda e: e.activation(out=o, in_=in_, func=func, **kw), reads, writes)

        def TT(eng, o, in0, in1, op, reads, writes):
            P.op(eng, lambda e: e.tensor_tensor(out=o, in0=in0, in1=in1, op=op), reads, writes)

        def TS(eng, o, in0, s1, s2, op0, op1, reads, writes):
            if s2 is None:
                P.op(eng, lambda e: e.tensor_scalar(out=o, in0=in0, scalar1=s1, scalar2=None, op0=op0), reads, writes)
            else:
                P.op(eng, lambda e: e.tensor_scalar(out=o, in0=in0, scalar1=s1, scalar2=s2, op0=op0, op1=op1), reads, writes)

        def STT(eng, o, in0, scalar, in1, op0, op1, reads, writes):
            P.op(eng, lambda e: e.scalar_tensor_tensor(out=o, in0=in0, scalar=scalar, in1=in1, op0=op0, op1=op1), reads, writes)

        def CP(eng, o, in_, reads, writes):
            if eng == 'act':
                P.op('act', lambda e: e.activation(out=o, in_=in_, func=AF.Copy), reads, writes)
            else:
                P.op(eng, lambda e: e.tensor_copy(o, in_), reads, writes)

        def DMA(q, o, in_, reads, writes, sem=None):
            P.dma(q, lambda e: e.dma_start(out=o, in_=in_), reads, writes, sem)

        def bkeys(pfx, lo, hi):
            ks = []
            if lo < 0 or hi > TPC:
                ks.append(pfx + 'h')
            lo2, hi2 = max(lo, 0), min(hi, TPC)
            if hi2 > lo2:
                for b in range(lo2 // 128, (hi2 - 1) // 128 + 1):
                    ks.append(f"{pfx}{b}")
            return ks

        ident = sb("ident", [128, 128], F32)
        identb = sb("identb", [128, 128], BF16)
        onesb = sb("onesb", [128, 128], BF16)
        epsc = sb("epsc", [128, 2], F32)
        XT = sb("XT", [128, 8, XW], F32)
        CT = sb("CT", [128, 8, CTX], F32)
        HT = sb("HT", [128, 8, XW], BF16)
        HC = sb("HC", [128, 8, CTX], BF16)
        MOD = sb("MOD", [128, 192, 2], F32)
        LS = sb("LS", [128, DEPTH, 2, 6, 8], F32)
        G1 = sb("G1", [128, DEPTH, 8], F32)
        G2 = sb("G2", [128, DEPTH, 8], F32)
        GFIN = sb("GFIN", [128, 8], F32)
        PW1B = sb("PW1B", [128, 2, 16], F32)
        DWW = sb("DWW", [128, 2, 8, 31], F32)
        CVEC = sb("CVEC", [128, 4, 2, 8], F32)
        C2 = sb("C2", [128, 2, 8], F32)
        MASKH = sb("MASKH", [128, 2 * HALO], F32)
        OHOT = sb("OHOT", [128, 2, 8], F32)
        LNG = sb("LNG", [128, 16], F32)
        PTAB = sb("PTAB", [128, 2, NP], F32)
        DEC = sb("DEC", [128, 8, NP], F32)
        M01 = sb("M01", [128, 2, 128], F32)
        XTAB = sb("XTAB", [128, 2, 2, 9], F32)
        COEF = sb("COEF", [128, 8, 9], F32)

        pbanks = [(es.enter_context(nc.psum_tensor(f"pb{i}", [128, 512], F32)), f"pb{i}") for i in range(6)]
        ptrs = [(es.enter_context(nc.psum_tensor(f"pt{i}", [128, 1024], BF16)), f"pt{i}") for i in range(2)]
        rot = Rot(pbanks)
        trot = Rot(ptrs)

        DMA('sp', ident[:], ident_in, [], ['ident'])
        CP('dve', identb[:], ident[:], ['ident'], ['identb'])
        P.op('dve', lambda e: e.memset(onesb[:], 1.0), [], ['onesb'])
        P.op('dve', lambda e: e.memset(epsc[:, 0:1], 1e-6), [], ['epsc'])
        P.op('dve', lambda e: e.memset(epsc[:, 1:2], 1e-5), [], ['epsc'])
        for t, src, key in ((G1, g1_in, 'G1'), (G2, g2_in, 'G2'), (GFIN, gfin_in, 'GFIN'), (PW1B, pw1b_in, 'PW1B'),
                            (DWW, dww_in, 'DWW'), (CVEC, cvec_in, 'CVEC'), (MASKH, maskh_in, 'MASKH'),
                            (OHOT, ohot_in, 'OHOT'), (LNG, lg_in, 'LNG'), (PTAB, ptab_in, 'PTAB'),
                            (M01, m01_in, 'M01'), (XTAB, xtab_in, 'XTAB')):
            DMA('sp', t[:], src, [], [key])

        pc_next = [0]

        class Piece:
            def __init__(self, idx, gbuf, gkey):
                self.idx, self.n, self.g, self.key = idx, pc_words[idx], gbuf, gkey
                self.boff = [0]
                for b in PLAN[idx]:
                    self.boff.append(self.boff[-1] + block_words(b))

            def kview(self, bi, r):
                ncols = PLAN[self.idx][bi][4]
                o = r * self.n + self.boff[bi]
                return self.g[o:o + 128 * ncols].rearrange("(p c) -> p c", c=ncols)

            def nview(self, bi, r, f0, nf):
                o = r * self.n + self.boff[bi] + f0 * 16384
                return self.g[o:o + nf * 16384].rearrange("(f p j) -> p f j", p=128, j=128)

        pending = [None]
        nhalf_box = [0]

        def prefetch_piece():
            idx = pc_next[0]
            if idx < len(PLAN) and PLAN[idx][0][0] == 'k':
                pending[0] = (PLAN[idx][0][:3], next_piece(PLAN[idx][0][:3], force=True))

        def next_piece(expect, gbuf=None, gkey='GG', force=False):
            if not force and pending[0] is not None:
                exp, pc = pending[0]
                pending[0] = None
                assert exp == expect, (exp, expect)
                return pc
            idx = pc_next[0]
            pc_next[0] += 1
            assert PLAN[idx][0][:3] == expect, (PLAN[idx], expect)
            n = pc_words[idx]
            gbuf = GG if gbuf is None else gbuf
            DMA('sp', GB[0:n].rearrange("(r c) -> r c", c=512), wpk_in[pc_off[idx]:pc_off[idx] + n].rearrange("(r c) -> r c", c=512),
                [], ['GB'])
            src = GB[0:n].rearrange("(r c) -> r c", c=512)
            dst = gbuf[0:NCORES * n].rearrange("(r c) -> r c", c=512)
            P.op('pool', lambda e: e.collective_compute("AllGather", ALU.bypass, replica_groups=RG, ins=[src], outs=[dst]),
                 ['GB'], [gkey])
            return Piece(idx, gbuf, gkey)

        with ExitStack() as ph:
            xs = [(sb(f"xs{i}", [128, D], F32, ph), f"xs{i}") for i in range(2)]
            xsr = Rot(xs)

            def load_T(src_rows, nrows, dstT, col0, wkeys):
                s, sk = xsr.next()
                DMA('sp', s[:nrows, :], src_rows, [], [sk])
                for half in range(2):
                    pb, pk = rot.next()
                    for q in range(4):
                        fc = half * 4 + q
                        TR(pb[:, q * 128:q * 128 + nrows], s[:nrows, fc * 128:(fc + 1) * 128], ident[:nrows, :nrows],
                           [sk, 'ident'], [pk])
                    src = pb[:].rearrange("p (q n) -> p q n", q=4)[:, :, :nrows]
                    CP('act' if half == 0 else 'dve', dstT[:, half * 4:half * 4 + 4, col0:col0 + nrows], src, [pk], wkeys)

            for t in range(TPC // 128):
                load_T(x_in[t * 128:(t + 1) * 128, :], 128, XT, HALO + t * 128, [f"XT{t}"])
            load_T(xh_in[0:HALO, :], HALO, XT, 0, ['XTh'])
            load_T(xh_in[HALO:2 * HALO, :], HALO, XT, HALO + TPC, ['XTh'])
            for t in range(CTX // 128):
                load_T(ctx_in[t * 128:(t + 1) * 128, :], 128, CT, t * 128, ['CT'])

            CCt = sb("CCt", [128, 8, 2], F32, ph)
            SC = sb("SC", [128, 8, 2], BF16, ph)
            MW = [(sb(f"MW{i}", [128, 8, 128], BF16, ph), f"MW{i}") for i in range(3)]
            MODL = sb("MODL", [128, 48], F32, ph)
            MODB = sb("MODB", [128, 192], F32, ph)
            DMA('sp', CCt[:], cc_in, [], ['CCt'])
            DMA('sp', MODB[:], modb_in, [], ['MODB'])
            ACT(SC[:], CCt[:], AF.Silu, ['CCt'], ['SC'])
            mwr = Rot(MW)
            pm, pmk = rot.next()
            for ql in range(24):
                w, wk = mwr.next()
                DMA('pool', w[:], modw_in[ql].rearrange("(kc p) n -> p kc n", p=128), [], [wk])
                for kc in range(8):
                    MM(pm[:, ql * 2:ql * 2 + 2], w[:, kc, :], SC[:, kc, :], kc == 0, kc == 7, [wk, 'SC'], [pmk])
            CP('dve', MODL[:], pm[:, 0:48], [pmk], ['MODL'])
            DMA('sp', mod_bounce, MODL[:], ['MODL'], ['mod_bounce'])
            P.op('pool', lambda e: e.collective_compute("AllGather", ALU.bypass, replica_groups=RG,
                                                        ins=[mod_bounce], outs=[mod_gath]),
                 ['mod_bounce'], ['mod_gath'])
            DMA('sp', MOD[:].rearrange("p (c q) s -> p c (q s)", c=NCORES),
                mod_gath.rearrange("(c p) n -> p c n", p=128), ['mod_gath'], ['MOD'])
            for s in range(2):
                TT('dve', MOD[:, :, s], MOD[:, :, s], MODB[:], ALU.add, ['MOD', 'MODB'], ['MOD'])
            for i in range(DEPTH):
                for s in range(2):
                    def mv(k):
                        return MOD[:, i * 48 + k * 8:i * 48 + k * 8 + 8, s]
                    STT('dve', LS[:, i, s, 0, :], mv(1), 1.0, G1[:, i, :], ALU.add, ALU.mult, ['MOD', 'G1'], ['LS'])
                    CP('dve', LS[:, i, s, 1, :], mv(0), ['MOD'], ['LS'])
                    CP('dve', LS[:, i, s, 2, :], mv(2), ['MOD'], ['LS'])
                    STT('dve', LS[:, i, s, 3, :], mv(4), 1.0, G2[:, i, :], ALU.add, ALU.mult, ['MOD', 'G2'], ['LS'])
                    CP('dve', LS[:, i, s, 4, :], mv(3), ['MOD'], ['LS'])
                    CP('dve', LS[:, i, s, 5, :], mv(5), ['MOD'], ['LS'])
            ACT(LNG[:], LNG[:], AF.Exp, ['LNG'], ['LNG'], scale=float(np.log(2.0)))
            TS('dve', LNG[:], LNG[:], -1.0, 1.0, ALU.mult, ALU.add, ['LNG'], ['LNG'])
            ACT(LNG[:], LNG[:], AF.Ln, ['LNG'], ['LNG'])
            if stage > 0:
                prefetch_piece()
        P.barrier()

        def norm_cols(ph, pfx):
            sq = sb(pfx + "sq", [128, 8, 512], BF16, ph)
            rs = sb(pfx + "rs", [128, 512], F32, ph)
            tm = Rot([(sb(pfx + f"tm{k}", [128, 512], F32, ph), pfx + f"tm{k}") for k in range(2)])
            return sq, rs, tm, pfx

        def norm_tile(scr, src3, n, Sap, Bap, dst3, rk, wk, skeys, epscol=0):
            sq, rs, tm, pfx = scr
            ACT(sq[:, :, :n], src3, AF.Square, rk, [pfx + 'sq'])
            pb, pk = rot.next()
            for fc in range(8):
                MM(pb[:, :n], onesb[:], sq[:, fc, :n], fc == 0, fc == 7, ['onesb', pfx + 'sq'], [pk])
            ACT(rs[:, :n], pb[:, :n], AF.Sqrt, [pk, 'epsc'], [pfx + 'rs'], bias=epsc[:, epscol:epscol + 1], scale=1.0 / D)
            P.op('dve', lambda e: e.reciprocal(rs[:, :n], rs[:, :n]), [pfx + 'rs'], [pfx + 'rs'])
            for fc in range(8):
                if Bap is None:
                    STT('dve', dst3[:, fc, :], src3[:, fc, :], Sap[:, fc:fc + 1], rs[:, :n], ALU.mult, ALU.mult,
                        rk + [pfx + 'rs'] + skeys, wk)
                else:
                    t, tk = tm.next()
                    STT('dve', t[:, :n], src3[:, fc, :], Sap[:, fc:fc + 1], rs[:, :n], ALU.mult, ALU.mult,
                        rk + [pfx + 'rs'] + skeys, [tk])
                    ACT(dst3[:, fc, :], t[:, :n], AF.Identity, [tk] + skeys, wk, bias=Bap[:, fc:fc + 1], scale=1.0)

        def norm_all(i, which, with_halo, do_ctx):
            with ExitStack() as ph:
                scr = norm_cols(ph, "n")
                k0 = 0 if which == 0 else 3
                if with_halo:
                    ranges = [(c, min(c + 512, XW)) for c in range(0, XW, 512)]
                else:
                    ranges = [(HALO + t * 512, HALO + (t + 1) * 512) for t in range(4)]
                for (c0, c1) in ranges:
                    n = c1 - c0
                    norm_tile(scr, XT[:, :, c0:c1], n, LS[:, i, 0, k0, :], LS[:, i, 0, k0 + 1, :], HT[:, :, c0:c1],
                              bkeys('XT', c0 - HALO, c1 - HALO), bkeys('HT', c0 - HALO, c1 - HALO), ['LS'])
                if do_ctx:
                    norm_tile(scr, CT[:, :, :], CTX, LS[:, i, 1, k0, :], LS[:, i, 1, k0 + 1, :], HC[:, :, :],
                              ['CT'], ['HC'], ['LS'])
            P.barrier()

        def conv_phase(i, do_ctx):
            j = i // 2
            TTK = 256
            with ExitStack() as ph:
                W1 = sb("W1", [128, 8, 2 * D], BF16, ph)
                W2 = sb("W2", [128, 8, D], BF16, ph)
                pc = next_piece(('k', 'pw1w', j))
                for r in range(NCORES):
                    DMA('pool', W1[:, r, :], pc.kview(0, r), ['GG'], ['W1'])
                pc = next_piece(('k', 'pw2w', j))
                for r in range(NCORES):
                    DMA('pool', W2[:, r, :], pc.kview(0, r), ['GG'], ['W2'])
                if nhalf_box[0] + 1 < stage:
                    prefetch_piece()
                UB = [sb(f"U{k}", [128, 8, TTK + 2 * HALO], BF16, ph) for k in range(2)]
                DW = sb("DW", [128, 8, TTK], F32, ph)
                DWB = Rot([(sb(f"DWB{k}", [128, TTK], BF16, ph), f"DWB{k}") for k in range(2)])
                DSQ = Rot([(sb(f"DSQ{k}", [128, TTK], BF16, ph), f"DSQ{k}") for k in range(2)])
                Z = sb("Z", [128, 8, TTK], BF16, ph)
                SIG = Rot([(sb(f"SIG{k}", [128, TTK + 2 * HALO], F32, ph), f"SIG{k}") for k in range(2)])
                MR = sb("MR", [128, 3, TTK], F32, ph)
                T2 = Rot([(sb(f"T2{k}", [128, TTK], F32, ph), f"T2{k}") for k in range(2)])
                PRD = {e: Rot([(sb(f"PRD{e}{k}", [128, TTK], BF16, ph), f"PRD{e}{k}") for k in range(n_)])
                       for e, n_ in (('dve', 4), ('act', 4))}
                TAP_ENG = ['dve', 'act', 'dve']
                dwb, lng_, lnb, pw2b = (CVEC[:, k, j, :] for k in range(4))
                for s in range(2):
                    TT('dve', C2[:, s, :], pw2b, LS[:, i, s, 2, :], ALU.mult, ['CVEC', 'LS'], ['C2'])
                tiles = [(0, tt) for tt in range(TPC // TTK)]
                if do_ctx:
                    tiles.append((1, 0))
                for ti, (s, tt) in enumerate(tiles):
                    U = UB[ti % 2]
                    uk = [f"U{ti % 2}_{fc}" for fc in range(8)]
                    if s == 0:
                        c0 = tt * TTK
                        n_in, u0 = TTK + 2 * HALO, 0
                        hsrc = lambda kc: HT[:, kc, c0:c0 + n_in]
                        hk = bkeys('HT', c0 - HALO, c0 + TTK + HALO)
                        xdst = lambda dc: XT[:, dc, HALO + c0:HALO + c0 + TTK]
                        xk = bkeys('XT', c0, c0 + TTK)
                    else:
                        n_in, u0 = CTX, HALO
                        hsrc = lambda kc: HC[:, kc, :]
                        hk = ['HC']
                        xdst = lambda dc: CT[:, dc, :]
                        xk = ['CT']
                        P.op('dve', lambda e, U=U: e.memset(U[:, :, 0:HALO], 0.0), [], uk)
                        P.op('dve', lambda e, U=U: e.memset(U[:, :, HALO + CTX:], 0.0), [], uk)
                    for fc in range(8):
                        pa, pak = rot.next()
                        pg, pgk = rot.next()
                        for kc in range(8):
                            MM(pa[:, :n_in], W1[:, kc, fc * 128:(fc + 1) * 128], hsrc(kc), kc == 0, kc == 7, ['W1'] + hk, [pak])
                        for kc in range(8):
                            MM(pg[:, :n_in], W1[:, kc, D + fc * 128:D + (fc + 1) * 128], hsrc(kc), kc == 0, kc == 7,
                               ['W1'] + hk, [pgk])
                        sg, sgk = SIG.next()
                        ACT(sg[:, :n_in], pg[:, :n_in], AF.Sigmoid, [pgk, 'PW1B'], [sgk], bias=PW1B[:, j, 8 + fc:9 + fc], scale=1.0)
                        STT('dve', U[:, fc, u0:u0 + n_in], pa[:, :n_in], PW1B[:, j, fc:fc + 1], sg[:, :n_in], ALU.add, ALU.mult,
                            [pak, sgk, 'PW1B'], [uk[fc]])
                    if s == 0 and tt == 0:
                        TT('dve', U[:, :, 0:HALO], U[:, :, 0:HALO], MASKH[:, 0:HALO].unsqueeze(1).broadcast_to([128, 8, HALO]),
                           ALU.mult, uk + ['MASKH'], uk)
                    if s == 0 and tt == TPC // TTK - 1:
                        TT('dve', U[:, :, TTK + HALO:], U[:, :, TTK + HALO:],
                           MASKH[:, HALO:].unsqueeze(1).broadcast_to([128, 8, HALO]), ALU.mult, uk + ['MASKH'], uk)
                    for fc in range(8):
                        pd, pdk = rot.next()
                        for k in range(31):
                            eng = TAP_ENG[(k + fc) % len(TAP_ENG)]
                            pr, prk = PRD[eng].next()
                            wk_ = DWW[:, j, fc, k:k + 1]
                            if eng == 'act':
                                ACT(pr[:], U[:, fc, k:k + TTK], AF.Identity, [uk[fc], 'DWW'], [prk], scale=wk_)
                            else:
                                TS(eng, pr[:], U[:, fc, k:k + TTK], wk_, None, ALU.mult, None, [uk[fc], 'DWW'], [prk])
                            MM(pd[:, :TTK], identb[:], pr[:], k == 0, k == 30, ['identb', prk], [pdk])
                        TS('dve', DW[:, fc, :], pd[:, :TTK], dwb[:, fc:fc + 1], None, ALU.add, None, [pdk, 'CVEC'], [f"DW{fc}"])
                    dwk = [f"DW{fc}" for fc in range(8)]
                    p1, p1k = rot.next()
                    p2, p2k = rot.next()
                    for fc in range(8):
                        b1_, b1k = DWB.next()
                        b2_, b2k = DSQ.next()
                        CP('act', b1_[:], DW[:, fc, :], [dwk[fc]], [b1k])
                        ACT(b2_[:], DW[:, fc, :], AF.Square, [dwk[fc]], [b2k])
                        MM(p1[:, :TTK], onesb[:], b1_[:], fc == 0, fc == 7, ['onesb', b1k], [p1k])
                        MM(p2[:, :TTK], onesb[:], b2_[:], fc == 0, fc == 7, ['onesb', b2k], [p2k])
                    ACT(MR[:, 0, :], p1[:, :TTK], AF.Identity, [p1k], ['MR'], scale=1.0 / D)
                    TT('dve', MR[:, 2, :], MR[:, 0, :], MR[:, 0, :], ALU.mult, ['MR'], ['MR'])
                    STT('dve', MR[:, 1, :], p2[:, :TTK], 1.0 / D, MR[:, 2, :], ALU.mult, ALU.subtract, [p2k, 'MR'], ['MR'])
                    ACT(MR[:, 1, :], MR[:, 1, :], AF.Sqrt, ['MR', 'epsc'], ['MR'], bias=epsc[:, 1:2], scale=1.0)
                    P.op('dve', lambda e: e.reciprocal(MR[:, 1, :], MR[:, 1, :]), ['MR'], ['MR'])
                    TT('dve', DW[:], DW[:], MR[:, 0, :].unsqueeze(1).broadcast_to([128, 8, TTK]), ALU.subtract, dwk + ['MR'], dwk)
                    TT('dve', DW[:], DW[:], MR[:, 1, :].unsqueeze(1).broadcast_to([128, 8, TTK]), ALU.mult, dwk + ['MR'], dwk)
                    for fc in range(8):
                        ACT(Z[:, fc, :], DW[:, fc, :], AF.Silu, [dwk[fc], 'CVEC'], ['Z'], bias=lnb[:, fc:fc + 1], scale=lng_[:, fc:fc + 1])
                    for dc in range(8):
                        pb, pk = rot.next()
                        for kc in range(8):
                            MM(pb[:, :TTK], W2[:, kc, dc * 128:(dc + 1) * 128], Z[:, kc, :], kc == 0, kc == 7, ['W2', 'Z'], [pk])
                        t2, t2k = T2.next()
                        ACT(t2[:], pb[:, :TTK], AF.Identity, [pk, 'LS', 'C2'], [t2k], bias=C2[:, s, dc:dc + 1],
                            scale=LS[:, i, s, 2, dc:dc + 1])
                        TT('dve', xdst(dc), xdst(dc), t2[:], ALU.add, xk + [t2k], xk)
            P.barrier()

        def ffn_phase(i, do_ctx):
            with ExitStack() as ph:
                slots = []
                for k in range(2):
                    slots.append(dict(
                        WG=sb(f"WG{k}", [128, 8, 5 * 128], BF16, ph), WU=sb(f"WU{k}", [128, 8, 5 * 128], BF16, ph),
                        WO=sb(f"WO{k}", [128, 5, D], BF16, ph), k=k))
                AT = [(sb(f"AT{k}", [128, 5, 512], BF16, ph), f"AT{k}") for k in range(2)]
                SG = Rot([(sb(f"SG{k}", [128, 512], BF16, ph), f"SG{k}") for k in range(2)])

                def load_group(g):
                    f0, f1 = FFN_GROUPS[g]
                    nf = f1 - f0
                    sl = slots[g % 2]
                    k = sl['k']
                    pc = next_piece(('k', 'ffnwin', i))
                    for r in range(NCORES):
                        DMA('pool', sl['WG'][:, r, :nf * 128], pc.kview(0, r), ['GG'], [f"WG{k}"])
                    for r in range(NCORES):
                        DMA('pool', sl['WU'][:, r, :nf * 128], pc.kview(1, r), ['GG'], [f"WU{k}"])
                    for r in range(NCORES):
                        DMA('pool', sl['WO'][:, 0:nf, r * 128:(r + 1) * 128], pc.nview(2, r, 0, nf), ['GG'], [f"WO{k}"])

                tiles = [(0, t) for t in range(4)]
                if do_ctx:
                    tiles.append((1, 0))
                load_group(0)
                for g in range(len(FFN_GROUPS)):
                    if g + 1 < len(FFN_GROUPS):
                        load_group(g + 1)
                        if g + 2 == len(FFN_GROUPS) and nhalf_box[0] + 1 < stage:
                            prefetch_piece()
                    f0, f1 = FFN_GROUPS[g]
                    nf = f1 - f0
                    sl = slots[g % 2]
                    k = sl['k']

                    def phaseA(ti):
                        s, t = tiles[ti]
                        at, atk = AT[ti % 2]
                        if s == 0:
                            n = 512
                            hsrc = lambda kc: HT[:, kc, HALO + t * 512:HALO + (t + 1) * 512]
                            hk = bkeys('HT', t * 512, (t + 1) * 512)
                        else:
                            n = CTX
                            hsrc = lambda kc: HC[:, kc, :]
                            hk = ['HC']
                        for f in range(nf):
                            pg, pgk = rot.next()
                            pu, puk = rot.next()
                            for kc in range(8):
                                MM(pg[:, :n], sl['WG'][:, kc, f * 128:(f + 1) * 128], hsrc(kc), kc == 0, kc == 7, [f"WG{k}"] + hk, [pgk])
                            for kc in range(8):
                                MM(pu[:, :n], sl['WU'][:, kc, f * 128:(f + 1) * 128], hsrc(kc), kc == 0, kc == 7, [f"WU{k}"] + hk, [puk])
                            sg, sgk = SG.next()
                            ACT(sg[:, :n], pg[:, :n], AF.Silu, [pgk], [sgk])
                            TT('dve', at[:, f, :n], pu[:, :n], sg[:, :n], ALU.mult, [puk, sgk], [atk])

                    def phaseB(ti):
                        s, t = tiles[ti]
                        at, atk = AT[ti % 2]
                        n = 512 if s == 0 else CTX
                        for dc in range(8):
                            pb, pk = rot.next()
                            for f in range(nf):
                                MM(pb[:, :n], sl['WO'][:, f, dc * 128:(dc + 1) * 128], at[:, f, :n], f == 0, f == nf - 1,
                                   [f"WO{k}", atk], [pk])
                            if s == 0:
                                xd = XT[:, dc, HALO + t * 512:HALO + (t + 1) * 512]
                                xk = bkeys('XT', t * 512, (t + 1) * 512)
                            else:
                                xd = CT[:, dc, :]
                                xk = ['CT']
                            STT('dve', xd, pb[:, :n], LS[:, i, s, 5, dc:dc + 1], xd, ALU.mult, ALU.add, [pk, 'LS'] + xk, xk)

                    for ti in range(len(tiles)):
                        phaseA(ti)
                        if ti > 0:
                            phaseB(ti - 1)
                    phaseB(len(tiles) - 1)
            P.barrier()

        KQ, KD, EP, CDC, KLF, KLB, KLCF = (0, 1), (2, 3), (4, 5), 6, 7, 23, 39

        def ret_phase(i, need_ctx_out):
            j = i // 2
            rot.items = pbanks[1:]
            with ExitStack() as ph:
                for dh in range(8):
                    ACT(DEC[:, dh, :], PTAB[:, 0, :], AF.Exp, ['PTAB', 'LNG'], ['DEC'], scale=LNG[:, j * 8 + dh:j * 8 + dh + 1])
                TT('dve', DEC[:], DEC[:], PTAB[:, 1, :].unsqueeze(1).broadcast_to([128, 8, NP]), ALU.mult, ['DEC', 'PTAB'], ['DEC'])
                for dh in range(8):
                    TS('dve', COEF[:, dh, :], XTAB[:, 0, dh // 4, :], LNG[:, j * 8 + dh:j * 8 + dh + 1], -80.0, ALU.mult, ALU.max,
                       ['XTAB', 'LNG'], ['COEF'])
                ACT(COEF[:], COEF[:], AF.Exp, ['COEF'], ['COEF'])
                for d in range(2):
                    TT('dve', COEF[:, d * 4:(d + 1) * 4, :], COEF[:, d * 4:(d + 1) * 4, :],
                       XTAB[:, 1, d, :].unsqueeze(1).broadcast_to([128, 4, 9]), ALU.mult, ['COEF', 'XTAB'], ['COEF'])

                QT = sb("QT", [128, 2, TPC], BF16, ph)
                KT = sb("KT", [128, 2, TPC], BF16, ph)
                V = sb("V", [128, 16, 512], BF16, ph)
                QC = sb("QC", [128, 2, CTX], BF16, ph)
                KC = sb("KC", [128, 2, CTX], BF16, ph)
                VC = sb("VC", [128, 2, 512], BF16, ph)
                SF = sb("SF", [128, 2, 512], F32, ph)
                SB = sb("SB", [128, 2, 512], F32, ph)
                S16D = [sb(f"S16_{k}", [128, 2, 512], BF16, ph) for k in range(2)]
                SSQ = sb("SSQ", [128, 2, 18], F32, ph)
                RSD = sb("RSD", [128, 2, 18], F32, ph)
                SCR = sb("SCR", [128, 4, 512], F32, ph)
                scr_rot = Rot([0, 1, 2, 3])
                PTT = Rot([(sb(f"PTT{k}", [128, 128], BF16, ph), f"PTT{k}") for k in range(2)])
                KS = Rot([(sb(f"KS{k}", [128, 256], BF16, ph), f"KS{k}") for k in range(2)])
                SGT = Rot([(sb(f"SGT{k}", [128, 512], BF16, ph), f"SGT{k}") for k in range(2)])
                T1O = Rot([(sb(f"T1O{k}", [128, 512], BF16, ph), f"T1O{k}") for k in range(2)])
                T1I = Rot([(sb(f"T1I{k}", [128, 512], BF16, ph), f"T1I{k}") for k in range(4)])
                YT = sb("YT", [128, 4, 512], BF16, ph)
                WA = sb("WA", [128, 4096], BF16, ph)
                WB = sb("WB", [128, 4096], BF16, ph)
                WA8 = WA[:].rearrange("p (a b) -> p a b", a=8)
                WB8 = WB[:].rearrange("p (a b) -> p a b", a=8)
                WA4 = WA[:].rearrange("p (a b) -> p a b", a=4)

                def lat_cols(n):
                    return slice(HALO + n * 128, HALO + (n + 1) * 128)

                (B0, B0k), (B1, B1k), (B2, B2k), (B3, B3k), (B4, B4k), (B5, B5k) = pbanks
                (PT0, PT0k), (PT1, PT1k) = ptrs

                def chunk_stages(h, d, ctx, n, has_state, make_state, gate_w, gate_wk):
                    dh = d * 4 + h
                    if ctx:
                        qs = lambda dc: QC[:, dc, n * 128:(n + 1) * 128]
                        ks_ = lambda dc: KC[:, dc, n * 128:(n + 1) * 128]
                        vn = VC[:, n, :]
                        hs = lambda kc: HC[:, kc, n * 128:(n + 1) * 128]
                        rk = ['QC', 'KC', 'VC']
                        hk = ['HC']
                        S_ = None
                    else:
                        qs = lambda dc: QT[:, dc, n * 128:(n + 1) * 128]
                        ks_ = lambda dc: KT[:, dc, n * 128:(n + 1) * 128]
                        vn = V[:, n, :]
                        hs = lambda kc: HT[:, kc, lat_cols(n)]
                        rk = ['QT', 'KT', 'V']
                        hk = [f"HT{n}"]
                        S_ = SF if d == 0 else SB
                    sk = 'SF' if d == 0 else 'SB'
                    S16 = S16D[d]
                    s16k = f"S16_{d}"
                    psc, psck = B0[:, d * 128:(d + 1) * 128], f"psc{d}"
                    pbg, pbgk = (B1, B1k) if d == 0 else (B2, B2k)
                    pt, ptk = (PT0, PT0k) if d == 0 else (PT1, PT1k)
                    pby, pbyk = B3, B3k
                    pkv = [(B4, B4k), (B5, B5k)]
                    for dc in range(2):
                        MM(psc, ks_(dc), qs(dc), dc == 0, dc == 1, rk, [psck])
                    for kc in range(8):
                        MM(pbg[:], hs(kc), gate_w[:, kc, :], kc == 0, kc == 7, hk + [gate_wk], [pbgk])
                    if make_state:
                        for dc in range(2):
                            TR(pt[:, dc * 128:(dc + 1) * 128], ks_(dc), identb[:], rk + ['identb'], [ptk])
                    yield
                    ptt, pttk = PTT.next()
                    STT('dve', ptt[:], psc, DEC[:, dh, KQ[d]:KQ[d] + 1], M01[:, d, :], ALU.mult, ALU.mult,
                        [psck, 'DEC', 'M01'], [pttk])
                    if make_state:
                        ksb, ksbk = KS.next()
                        ACT(ksb[:], pt[:, 0:256], AF.Identity, [ptk, 'DEC'], [ksbk], scale=DEC[:, dh, KD[d]:KD[d] + 1])
                    sg, sgk = SGT.next()
                    ACT(sg[:], pbg[:], AF.Silu, [pbgk], [sgk])
                    yield
                    MM(pby[:], ptt[:], vn, True, not has_state, [pttk] + rk, [pbyk])
                    if has_state:
                        for dc in range(2):
                            MM(pby[:], qs(dc), S16[:, dc, :], False, dc == 1, rk + [s16k], [pbyk])
                    if make_state:
                        for dc in range(2):
                            MM(pkv[dc][0][:], ksb[:, dc * 128:(dc + 1) * 128], vn, True, True, [ksbk] + rk, [pkv[dc][1]])
                    yield
                    if make_state:
                        for dc in range(2):
                            if ctx:
                                CP('act', S16[:, dc, :], pkv[dc][0][:], [pkv[dc][1]], [s16k])
                            else:
                                STT('dve', S_[:, dc, :], S_[:, dc, :], DEC[:, dh, CDC:CDC + 1], pkv[dc][0][:], ALU.mult, ALU.add,
                                    [sk, 'DEC', pkv[dc][1]], [sk])
                                CP('act', S16[:, dc, :], S_[:, dc, :], [sk], [s16k])
                    to, tok_ = T1O.next()
                    sidx = (16 + n) if ctx else n
                    ACT(to[:], pby[:], AF.Square, [pbyk, 'SSQ0'], [tok_, f"SSQ{d}_{sidx}"], accum=SSQ[:, d, sidx:sidx + 1])
                    TT('dve', to[:], pby[:], sg[:], ALU.mult, [pbyk, sgk], [tok_])
                    DMA('sp', u_dram[d, sidx], to[:], [tok_], [f"ud{d}_{sidx}"], sem='t1w')
                    yield

                def chunk_step(*a):
                    for _ in chunk_stages(*a):
                        pass

                def chunk_pair(ga, gb):
                    next(ga); next(gb); next(ga); next(gb); next(ga); next(ga); next(gb); next(gb)

                pcw_next = next_piece(('k', 'retwin', j))
                for h in range(4):
                    pcw = pcw_next
                    for r in range(NCORES):
                        DMA('pool', WA8[:, r, :], pcw.kview(0, r)[:, 0:512], ['GG'], ['WA'])
                    for r in range(NCORES):
                        DMA('pool', WB8[:, r, :], pcw.kview(0, r)[:, 512:1024], ['GG'], ['WB'])
                    if h % 2 == 0:
                        pco = next_piece(('n', 'retwout', j), GG2, 'GG2')
                    for t in range(4):
                        DMA('sp', SCR[:, 0:2, :], rope_in[:, :, t * 512:(t + 1) * 512], [], ['SCR0', 'SCR1'])
                        hk = bkeys('HT', t * 512, (t + 1) * 512)
                        for (coff, dstT, dkey) in ((0, QT, 'QT'), (256, KT, 'KT')):
                            pp = [rot.next() for _ in range(2)]
                            for dc in range(2):
                                for kc in range(8):
                                    MM(pp[dc][0][:], WA8[:, kc, coff + dc * 128:coff + (dc + 1) * 128],
                                       HT[:, kc, HALO + t * 512:HALO + (t + 1) * 512], kc == 0, kc == 7, ['WA'] + hk, [pp[dc][1]])
                            (pA, pAk), (pB, pBk) = pp
                            cs, sn = SCR[:, 0, :], SCR[:, 1, :]
                            t1_, t2_ = SCR[:, 2, :], SCR[:, 3, :]
                            dsl = slice(t * 512, (t + 1) * 512)
                            TT('dve', t1_, pA[:], cs, ALU.mult, [pAk, 'SCR0'], ['SCR2'])
                            TT('dve', t2_, pB[:], sn, ALU.mult, [pBk, 'SCR1'], ['SCR3'])
                            TT('dve', dstT[:, 0, dsl], t1_, t2_, ALU.subtract, ['SCR2', 'SCR3'], [dkey])
                            TT('dve', t1_, pB[:], cs, ALU.mult, [pBk, 'SCR0'], ['SCR2'])
                            TT('dve', t2_, pA[:], sn, ALU.mult, [pAk, 'SCR1'], ['SCR3'])
                            TT('dve', dstT[:, 1, dsl], t1_, t2_, ALU.add, ['SCR2', 'SCR3'], [dkey])
                    for (coff, dstT, dkey) in ((0, QC, 'QC'), (256, KC, 'KC')):
                        for dc in range(2):
                            pb, pk = rot.next()
                            for kc in range(8):
                                MM(pb[:, :CTX], WA8[:, kc, coff + dc * 128:coff + (dc + 1) * 128], HC[:, kc, :], kc == 0, kc == 7,
                                   ['WA', 'HC'], [pk])
                            CP('act', dstT[:, dc, :], pb[:, :CTX], [pk], [dkey])
                    for n in range(16):
                        pb, pk = rot.next()
                        for kc in range(8):
                            MM(pb[:], HT[:, kc, lat_cols(n)], WB8[:, kc, :], kc == 0, kc == 7, ['WB', f"HT{n}"], [pk])
                        CP('act', V[:, n, :], pb[:], [pk], ['V'])
                    for n in range(2):
                        pb, pk = rot.next()
                        for kc in range(8):
                            MM(pb[:], HC[:, kc, n * 128:(n + 1) * 128], WB8[:, kc, :], kc == 0, kc == 7, ['WB', 'HC'], [pk])
                        CP('act', VC[:, n, :], pb[:], [pk], ['VC'])
                    for r in range(NCORES):
                        DMA('pool', WA8[:, r, :], pcw.kview(0, r)[:, 1024:1536], ['GG'], ['WA'])
                    for r in range(NCORES):
                        DMA('pool', WB8[:, r, :], pcw.kview(0, r)[:, 1536:2048], ['GG'], ['WB'])

                    def pass1(ctx):
                        banks = [rot.next() for _ in range(4)]
                        nch = 2 if ctx else 16
                        for n in range(nch):
                            pt, ptk = trot.next()
                            for dc in range(2):
                                src = KC[:, dc, n * 128:(n + 1) * 128] if ctx else KT[:, dc, n * 128:(n + 1) * 128]
                                TR(pt[:, dc * 128:(dc + 1) * 128], src, identb[:], ['KC' if ctx else 'KT', 'identb'], [ptk])
                            vn = VC[:, n, :] if ctx else V[:, n, :]
                            for d in range(2):
                                col = (KLCF + n if d == 0 else KLB + n) if ctx else (KLF + n if d == 0 else KLB + n)
                                ksb, ksbk = KS.next()
                                ACT(ksb[:], pt[:, 0:256], AF.Identity, [ptk, 'DEC'], [ksbk], scale=DEC[:, d * 4 + h, col:col + 1])
                                for dc in range(2):
                                    b, bk = banks[d * 2 + dc]
                                    MM(b[:], ksb[:, dc * 128:(dc + 1) * 128], vn, n == 0, n == nch - 1,
                                       [ksbk, 'VC' if ctx else 'V'], [bk])
                        return banks

                    banks = pass1(True)
                    for d in range(2):
                        S_, sk = (SF, 'SF') if d == 0 else (SB, 'SB')
                        for dc in range(2):
                            b, bk = banks[d * 2 + dc]
                            TS('dve', S_[:, dc, :], b[:], COEF[:, d * 4 + h, 0:1], None, ALU.mult, None, [bk, 'COEF'], [sk])
                    banks = pass1(False)
                    ST_L = [SCR[:, 0, :], SCR[:, 1, :], SCR[:, 2, :], SCR[:, 3, :]]
                    for q in range(4):
                        b, bk = banks[q]
                        CP('dve', ST_L[q], b[:], [bk], [f"SCR{q}"])

                    P.op('dve', lambda e: e.memset(SSQ[:], 0.0), [f"SSQ{d_}_{q_}" for d_ in range(2) for q_ in range(18)], ['SSQ0'])
                    if need_ctx_out:
                        for n in range(2):
                            chunk_step(h, 0, True, n, n == 1, n == 0, WA8, 'WA')
                        for n in (1, 0):
                            chunk_step(h, 1, True, n, n == 0, n == 1, WB8, 'WB')

                    for q in range(4):
                        DMA('sp', st_bounce[q * 128:(q + 1) * 128, :], SCR[:, q, :], [f"SCR{q}"], ['GB'])
                    P.op('pool', lambda e: e.collective_compute("AllGather", ALU.bypass, replica_groups=RG,
                                                                ins=[st_bounce], outs=[st_gath]),
                         ['GB'], ['GG'])
                    for d in range(2):
                        S_, sk = (SF, 'SF') if d == 0 else (SB, 'SB')
                        for dc in range(2):
                            for cp_ in range(NCORES):
                                sl = scr_rot.next()
                                r0 = (cp_ * 4 + d * 2 + dc) * 128
                                DMA('sp', SCR[:, sl, :], st_gath[r0:r0 + 128, :], ['GG'], [f"SCR{sl}"])
                                STT('dve', S_[:, dc, :], SCR[:, sl, :], COEF[:, d * 4 + h, 1 + cp_:2 + cp_], S_[:, dc, :],
                                    ALU.mult, ALU.add, [f"SCR{sl}", 'COEF', sk], [sk])

                    if h < 3:
                        pcw_next = next_piece(('k', 'retwin', j))
                    elif nhalf_box[0] + 1 < stage:
                        prefetch_piece()
                    for dc in range(2):
                        CP('act', S16D[0][:, dc, :], SF[:, dc, :], ['SF'], ['S16_0'])
                        CP('act', S16D[1][:, dc, :], SB[:, dc, :], ['SB'], ['S16_1'])
                    for st_ in range(16):
                        chunk_pair(chunk_stages(h, 0, False, st_, True, st_ < 15, WA8, 'WA'),
                                   chunk_stages(h, 1, False, 15 - st_, True, st_ < 15, WB8, 'WB'))
                    allssq = [f"SSQ{d_}_{q_}" for d_ in range(2) for q_ in range(18)]
                    for d in range(2):
                        ACT(RSD[:, d, :], SSQ[:, d, :], AF.Sqrt, allssq + ['DEC', 'SSQ0'], ['RSD'],
                            bias=DEC[:, d * 4 + h, EP[d]:EP[d] + 1], scale=1.0 / 512)
                    P.op('dve', lambda e: e.reciprocal(RSD[:], RSD[:]), ['RSD'], ['RSD'])
                    for r in range(NCORES):
                        DMA('pool', WA4[:, :, r * 128:(r + 1) * 128], pco.nview(0, r, (h % 2) * 4, 4), ['GG2'], ['WA'])

                    def merge(sidx, col0):
                        t1, t1k = T1I.next()
                        t2, t2k = T1I.next()
                        DMA('sp', t1[:], u_dram[0, sidx], [f"ud0_{sidx}"], [t1k])
                        DMA('sp', t2[:], u_dram[1, sidx], [f"ud1_{sidx}"], [t2k])
                        TS('dve', t1[:], t1[:], RSD[:, 0, sidx:sidx + 1], None, ALU.mult, None, [t1k, 'RSD'], [t1k])
                        STT('dve', t1[:], t2[:], RSD[:, 1, sidx:sidx + 1], t1[:], ALU.mult, ALU.add, [t1k, t2k, 'RSD'], [t1k])
                        pt, ptk = trot.next()
                        for vc in range(4):
                            TR(pt[:, vc * 128:(vc + 1) * 128], t1[:, vc * 128:(vc + 1) * 128], identb[:], [t1k, 'identb'], [ptk])
                        CP('act', YT[:, :, col0:col0 + 128], pt[:, 0:512].rearrange("p (a b) -> p a b", a=4), [ptk], ['YT'])

                    for t in range(4):
                        for q in range(4):
                            merge(t * 4 + q, q * 128)
                        xk = bkeys('XT', t * 512, (t + 1) * 512)
                        for dc in range(8):
                            pb, pk = rot.next()
                            for vc in range(4):
                                MM(pb[:], WA4[:, vc, dc * 128:(dc + 1) * 128], YT[:, vc, :], vc == 0, vc == 3, ['WA', 'YT'], [pk])
                            xd = XT[:, dc, HALO + t * 512:HALO + (t + 1) * 512]
                            STT('dve', xd, pb[:], LS[:, i, 0, 2, dc:dc + 1], xd, ALU.mult, ALU.add, [pk, 'LS'] + xk, xk)
                    if need_ctx_out:
                        for q in range(2):
                            merge(16 + q, q * 128)
                        for dc in range(8):
                            pb, pk = rot.next()
                            for vc in range(4):
                                MM(pb[:, :CTX], WA4[:, vc, dc * 128:(dc + 1) * 128], YT[:, vc, 0:CTX], vc == 0, vc == 3, ['WA', 'YT'], [pk])
                            xd = CT[:, dc, :]
                            STT('dve', xd, pb[:, :CTX], LS[:, i, 1, 2, dc:dc + 1], xd, ALU.mult, ALU.add, [pk, 'LS', 'CT'], ['CT'])
            rot.items = pbanks
            P.barrier()

        def halo_exchange():
            with ExitStack() as ph:
                HB = sb("HB", [128, 8, 2 * HALO], F32, ph)
                HG = sb("HG", [128, NCORES, 8 * 2 * HALO], F32, ph)
                CP('dve', HB[:, :, 0:HALO], XT[:, :, HALO:2 * HALO], ['XT0'], ['HB'])
                CP('dve', HB[:, :, HALO:], XT[:, :, TPC:TPC + HALO], ['XT15'], ['HB'])
                DMA('sp', halo_bounce, HB[:].rearrange("p a b -> p (a b)"), ['HB'], ['halo_bounce'])
                P.op('pool', lambda e: e.collective_compute("AllGather", ALU.bypass, replica_groups=RG,
                                                            ins=[halo_bounce], outs=[halo_gath]),
                     ['halo_bounce'], ['halo_gath'])
                DMA('sp', HG[:], halo_gath.rearrange("(c p) n -> p c n", p=128), ['halo_gath'], ['HG'])
                P.op('dve', lambda e: e.memset(XT[:, :, 0:HALO], 0.0), [], ['XTh'])
                P.op('dve', lambda e: e.memset(XT[:, :, HALO + TPC:], 0.0), [], ['XTh'])
                for cp_ in range(NCORES):
                    hg = HG[:, cp_, :].rearrange("p (a b) -> p a b", a=8)
                    xl = XT[:, :, 0:HALO]
                    xr = XT[:, :, HALO + TPC:]
                    STT('dve', xl, hg[:, :, HALO:], OHOT[:, 0, cp_:cp_ + 1], xl, ALU.mult, ALU.add, ['HG', 'OHOT', 'XTh'], ['XTh'])
                    STT('dve', xr, hg[:, :, 0:HALO], OHOT[:, 1, cp_:cp_ + 1], xr, ALU.mult, ALU.add, ['HG', 'OHOT', 'XTh'], ['XTh'])
            P.barrier()

        nhalf = 0
        for i in range(DEPTH):
            nhalf_box[0] = nhalf
            last = i == DEPTH - 1
            if nhalf >= stage:
                break
            if i % 2 == 0:
                if i > 0:
                    halo_exchange()
                norm_all(i, 0, True, not last)
                conv_phase(i, not last)
            else:
                norm_all(i, 0, False, True)
                ret_phase(i, not last)
            nhalf += 1
            nhalf_box[0] = nhalf
            if nhalf >= stage:
                break
            norm_all(i, 1, False, not last)
            ffn_phase(i, not last)
            nhalf += 1

        with ExitStack() as ph:
            YF = sb("YF", [128, 8, 512], F32, ph)
            OS = Rot([(sb(f"OS{k}", [128, D], F32, ph), f"OS{k}") for k in range(2)])
            scr = norm_cols(ph, "o")

            def out_T(srcT, col0, n, dst_rows, gain, rk, dkey):
                if final_norm:
                    norm_tile(scr, srcT[:, :, col0:col0 + n], n, gain, None, YF[:, :, :n], rk, ['YF'], ['GFIN'])
                else:
                    CP('dve', YF[:, :, :n], srcT[:, :, col0:col0 + n], rk, ['YF'])
                for tb in range(n // 128):
                    o, ok = OS.next()
                    for half in range(2):
                        pb, pk = rot.next()
                        for q in range(4):
                            fc = half * 4 + q
                            TR(pb[:, q * 128:(q + 1) * 128], YF[:, fc, tb * 128:(tb + 1) * 128], ident[:], ['YF', 'ident'], [pk])
                        CP('act' if half == 0 else 'dve', o[:, half * 512:(half + 1) * 512], pb[:], [pk], [ok])
                    DMA('sp', dst_rows[tb * 128:(tb + 1) * 128, :], o[:], [ok], [dkey], sem='outsem')

            for t in range(4):
                out_T(XT, HALO + t * 512, 512, out[t * 512:(t + 1) * 512, :], GFIN, bkeys('XT', t * 512, (t + 1) * 512), 'out')
            if dbg:
                out_T(CT, 0, CTX, cdbg, GFIN, ['CT'], 'cdbg')
        P.barrier()
        P.emit(es)
    global LASTP
    LASTP = P
    return nc


def _fm(v):
    v = np.asarray(v, np.float32)
    lead = v.shape[:-1]
    return np.ascontiguousarray(np.moveaxis(v.reshape(lead + (8, 128)), -1, 0))


def rsh(w, c):
    k = w.shape[1] // NCORES
    return np.ascontiguousarray(w[:, c * k:(c + 1) * k, :])


def csh(w, c):
    k = w.shape[2] // NCORES
    return np.ascontiguousarray(w[:, :, c * k:(c + 1) * k])


def _prep(inp):
    f = lambda k: np.asarray(inp[k], np.float32)
    x = f('x')[0]
    ctx = np.ascontiguousarray(f('ctx')[0])
    c = f('c')[0]
    c_ctx = f('c_ctx')
    mod_w, mod_b = f('mod_w'), f('mod_b')
    cc = np.ascontiguousarray(np.stack([c.reshape(8, 128).T, c_ctx.reshape(8, 128).T], axis=-1))
    modb = np.ascontiguousarray(mod_b.reshape(DEPTH * 48, 128).T)
    g1 = np.ascontiguousarray(_fm(f('norm1_g')))
    g2 = np.ascontiguousarray(_fm(f('norm2_g')))
    gfin = np.ascontiguousarray(_fm(f('final_norm_g')))
    pw1b = np.ascontiguousarray(np.moveaxis(f('conv_pw1_b').reshape(2, 16, 128), -1, 0))
    dww = np.ascontiguousarray(np.transpose(f('conv_dw_w').reshape(2, 31, 8, 128), (3, 0, 2, 1)))
    cvec = np.ascontiguousarray(np.stack([_fm(f('conv_dw_b')), _fm(f('conv_ln_g')), _fm(f('conv_ln_b')),
                                          _fm(f('conv_pw2_b'))], axis=1))
    w_in = f('ret_w_in')
    hp = np.concatenate([np.arange(0, 64), np.arange(128, 192), np.arange(64, 128), np.arange(192, 256)])
    perm = []
    for h in range(4):
        perm += [h * 256 + hp, 1024 + h * 256 + hp, 2048 + h * 512 + np.arange(512), 4096 + h * 512 + np.arange(512),
                 6144 + h * 512 + np.arange(512)]
    perm = np.concatenate(perm)
    retwin = np.ascontiguousarray(w_in[:, :, perm])
    lg = np.ascontiguousarray(np.tile(f('ret_log2_eps').reshape(1, 16), (128, 1)))
    ident = np.eye(128, dtype=np.float32)
    p = np.arange(128, dtype=np.float64)
    ptab = np.zeros((128, 2, NP), np.float64)
    cols = [-(p + 1), -(128 - p), 127 - p, p, -2 * (p + 1), -2 * (128 - p), np.full(128, 128.0)]
    mult = [1 / 16, 1 / 16, 1 / 16, 1 / 16, 1e-6, 1e-6, 1.0]
    for n in range(16):
        cols.append(2047 - 128 * n - p)
        mult.append(1 / 16)
    for n in range(16):
        cols.append(128 * n + p)
        mult.append(1 / 16)
    for n in range(2):
        cols.append(255 - 128 * n - p)
        mult.append(1 / 16)
    for k in range(NP):
        ptab[:, 0, k] = cols[k]
        ptab[:, 1, k] = mult[k]
    ptab = ptab.astype(np.float32)
    jj = np.arange(128)[:, None]
    ii = np.arange(128)[None, :]
    m01 = np.ascontiguousarray(np.stack([(ii >= jj), (jj >= ii)], axis=1).astype(np.float32))
    quarter = 64
    inv = (np.float32(10000.0) ** (-np.arange(quarter, dtype=np.float32) / np.float32(quarter))).astype(np.float32)
    ws = dict(pw1w=f('conv_pw1_w'), pw2w=f('conv_pw2_w'), retwin=retwin, retwout=f('ret_w_out'), ffnwin=f('ffn_w_in'),
              ffnwout=f('ffn_w_out'))
    maps = []
    for cidx in range(NCORES):
        t0 = cidx * TPC
        tok = np.arange(t0, t0 + TPC)
        row = (tok // 64).astype(np.float32)
        col = (tok % 64).astype(np.float32)
        ang = np.concatenate([inv[:, None] * row[None, :], inv[:, None] * col[None, :]], axis=0).astype(np.float32)
        rope = np.ascontiguousarray(np.stack([np.cos(ang), np.sin(ang)], axis=1).astype(np.float32))
        xh = np.zeros((2 * HALO, D), np.float32)
        maskh = np.zeros((128, 2 * HALO), np.float32)
        if cidx > 0:
            xh[:HALO] = x[t0 - HALO:t0]
            maskh[:, :HALO] = 1.0
        if cidx < NCORES - 1:
            xh[HALO:] = x[t0 + TPC:t0 + TPC + HALO]
            maskh[:, HALO:] = 1.0
        modw = np.empty((24, D, 128), np.float32)
        for ql in range(24):
            q = 24 * cidx + ql
            layer, k, fc = q // 48, (q % 48) // 8, q % 8
            modw[ql] = mod_w[layer][:, k * 1024 + fc * 128:k * 1024 + (fc + 1) * 128]
        xtab = np.zeros((128, 2, 2, 9), np.float32)
        for s in range(9):
            if s == 0:
                xtab[:, 0, 0, s] = 2048.0 * cidx
                xtab[:, 1, 0, s] = 1.0
                xtab[:, 0, 1, s] = 2048.0 * (NCORES - 1 - cidx)
                xtab[:, 1, 1, s] = 1.0
            else:
                cp_ = s - 1
                if cp_ < cidx:
                    xtab[:, 0, 0, s] = 2048.0 * (cidx - 1 - cp_)
                    xtab[:, 1, 0, s] = 1.0
                if cp_ > cidx:
                    xtab[:, 0, 1, s] = 2048.0 * (cp_ - cidx - 1)
                    xtab[:, 1, 1, s] = 1.0
        ohot = np.zeros((128, 2, 8), np.float32)
        if cidx > 0:
            ohot[:, 0, cidx - 1] = 1.0
        if cidx < NCORES - 1:
            ohot[:, 1, cidx + 1] = 1.0
        maps.append({
            "x": np.ascontiguousarray(x[t0:t0 + TPC]), "xh": xh, "ctx": ctx, "cc": cc, "modw": modw, "modb": modb,
            "g1": g1, "g2": g2, "gfin": gfin, "wpk": pack_pieces(ws, cidx),
            "pw1b": pw1b, "dww": dww, "cvec": cvec,
            "lg": lg, "ident": ident,
            "ptab": ptab, "m01": m01,
            "rope": rope, "xtab": xtab, "ohot": ohot, "maskh": maskh,
        })
    return maps


def kernel(**inputs):
    maps = _prep(inputs)
    nc = build_program()
    res = run_bass_kernel_spmd(nc, maps, core_ids=list(range(NCORES)))
    out = np.concatenate([np.asarray(r["out"]) for r in res.results], axis=0)
    return out.reshape(1, SEQ, D).astype(np.float32)
```
